# Optimizing a Trainium2 kernel written in Bass

```python
import jax, jax.numpy as jnp
from jax import lax
import numpy as np

D_MODEL = 1024
BATCH = 1
SEQ = 16384
DEPTH = 2
DEC_BATCH = 32
DEC_SEQ = 8
PAST_LEN = 16384
PAGE_SIZE = 128

LRU_WIDTH = D_MODEL // 2
LRU_BLOCKS = 8
LRU_BD = LRU_WIDTH // LRU_BLOCKS
CONV_W = 4
LRU_C = 8.0
HG_HEADS = 4
HG_DK = 128
HG_DV = 128
HG_WIDTH = HG_HEADS * HG_DK
HG_CHUNK = 64
DIL_GROUPS = ((128, 1), (512, 4), (2048, 16))
N_GROUPS = 3
DIL_HPG = 4
DIL_DH = 64
DIL_HEADS = N_GROUPS * DIL_HPG
DIL_WIDTH = DIL_HEADS * DIL_DH
DIL_OUT = DIL_HPG * DIL_DH
DIL_QB = 128
N_BRANCH = 3
D_FF = 4 * D_MODEL
EPS = 1e-6
NEG = -1e30
SPLIT_SIZES = (LRU_WIDTH,) + (HG_WIDTH,) * 4 + (DIL_WIDTH,) * 3 + (N_BRANCH * D_MODEL,)
N_IN = sum(SPLIT_SIZES)
F32 = jnp.float32

kernel_name = 'hybrid_rglru_hgrn2_dilated_decode_step'


def _rmsnorm(x, g):
    xf = x.astype(F32)
    y = xf * lax.rsqrt(jnp.mean(xf * xf, axis=-1, keepdims=True) + EPS)
    return (y * g.astype(F32)).astype(x.dtype)


def _masked_softmax(s, valid):
    s = jnp.where(valid, s, NEG)
    m = jnp.max(s, axis=-1, keepdims=True)
    p = jnp.where(valid, jnp.exp(s - m), 0.0)
    l = jnp.sum(p, axis=-1, keepdims=True)
    return p / l, (m + jnp.log(l))[..., 0]


def _lin_comb(left, right):
    a1, b1 = left
    a2, b2 = right
    return a1 * a2, a2 * b1 + b2


def _rglru(u, conv_st, h0, pos0, conv_w, conv_b, wr, br, wi, bi, lam):
    bsz, t_len, width = u.shape
    ext = jnp.concatenate([conv_st.astype(u.dtype), u], axis=1)
    xc = conv_b + sum(ext[:, j:j + t_len] * conv_w[j] for j in range(CONV_W))
    conv_new = ext[:, t_len:]
    xf = xc.astype(F32)
    xb = xf.reshape(bsz, t_len, LRU_BLOCKS, LRU_BD)
    r = jax.nn.sigmoid(jnp.einsum('btnd,nde->btne', xb, wr.astype(F32)).reshape(bsz, t_len, width) + br.astype(F32))
    i = jax.nn.sigmoid(jnp.einsum('btnd,nde->btne', xb, wi.astype(F32)).reshape(bsz, t_len, width) + bi.astype(F32))
    log_a = -LRU_C * r * jax.nn.softplus(-lam.astype(F32))
    pos = pos0 + jnp.arange(t_len)
    mult = jnp.where((pos == 0)[None, :, None], 1.0, jnp.sqrt(-jnp.expm1(2.0 * log_a)))
    a_cum, b_cum = lax.associative_scan(_lin_comb, (jnp.exp(log_a), mult * i * xf), axis=1)
    hs = b_cum + a_cum * h0.astype(F32)[:, None]
    return hs.astype(u.dtype), conv_new, hs[:, -1].astype(h0.dtype)


def _chunk_gated_scan(q, k, v, log_f, s0):
    bsz, t_len, nh, dk = q.shape
    dv = v.shape[-1]
    c = min(HG_CHUNK, t_len)
    n = -(-t_len // c)
    pad = n * c - t_len

    def to_chunks(z):
        z = jnp.pad(z, ((0, 0), (0, pad), (0, 0), (0, 0)))
        return z.reshape(bsz, n, c, nh, z.shape[-1]).swapaxes(0, 1)

    causal = jnp.tril(jnp.ones((c, c), dtype=bool))[None, :, :, None, None]

    def step(state, blk):
        qc, kc, vc, gc = blk
        b = jnp.cumsum(gc, axis=1)
        diff = b[:, :, None] - b[:, None, :]
        decay = jnp.where(causal, jnp.exp(jnp.where(causal, diff, 0.0)), 0.0)
        attn = jnp.einsum('bthd,btshd,bshd->bhts', qc, decay, kc)
        o = jnp.einsum('bhts,bshv->bthv', attn, vc) + jnp.einsum('bthd,bhdv->bthv', qc * jnp.exp(b), state)
        b_last = b[:, -1]
        state = jnp.exp(b_last)[..., None] * state + jnp.einsum('bshd,bshv->bhdv', kc * jnp.exp(b_last[:, None] - b), vc)
        return state, o

    s_fin, o = lax.scan(step, s0, (to_chunks(q), to_chunks(k), to_chunks(v), to_chunks(log_f)))
    o = o.swapaxes(0, 1).reshape(bsz, n * c, nh, dv)[:, :t_len]
    return o, s_fin


def _hgrn2(q_raw, f_raw, i_raw, g_raw, s0, lb, norm_g):
    bsz, t_len, _ = q_raw.shape
    shp = (bsz, t_len, HG_HEADS, HG_DK)
    q = jax.nn.silu(q_raw.astype(F32)).reshape(shp)
    zf = f_raw.astype(F32)
    lbf = lb.astype(F32)
    log_f = jnp.log(lbf + (1.0 - lbf) * jax.nn.sigmoid(zf)).reshape(shp)
    k = ((1.0 - lbf) * jax.nn.sigmoid(-zf)).reshape(shp)
    v = i_raw.astype(F32).reshape(bsz, t_len, HG_HEADS, HG_DV)
    o, s_fin = _chunk_gated_scan(q, k, v, log_f, s0.astype(F32))
    o = o * lax.rsqrt(jnp.mean(o * o, axis=-1, keepdims=True) + EPS) * norm_g.astype(F32)
    o = o.reshape(bsz, t_len, HG_WIDTH) * jax.nn.silu(g_raw.astype(F32))
    return o.astype(q_raw.dtype), s_fin.astype(s0.dtype)


def _dilated_prompt(q, k, v, window, dil):
    bsz, s_len, nh, dh = q.shape
    reach = window // dil
    span = dil * DIL_QB
    s_pad = -(-s_len // span) * span
    m_len = s_pad // dil
    nb = m_len // DIL_QB

    def to_sub(z):
        z = jnp.pad(z.astype(F32), ((0, 0), (0, s_pad - s_len), (0, 0), (0, 0)))
        z = z.reshape(bsz, m_len, dil, nh, dh).transpose(0, 2, 1, 3, 4)
        return z.reshape(bsz, dil, nb, DIL_QB, nh, dh)

    def with_prev(z):
        prev = jnp.concatenate([jnp.zeros_like(z[:, :, :1]), z[:, :, :-1]], axis=2)
        return jnp.concatenate([prev, z], axis=3)

    qs = to_sub(q)
    kk = with_prev(to_sub(k))
    vv = with_prev(to_sub(v))
    s = jnp.einsum('brnqhd,brnkhd->brnhqk', qs, kk)
    qi = jnp.arange(DIL_QB)[:, None]
    kj = jnp.arange(2 * DIL_QB)[None, :] - DIL_QB
    dist = qi - kj
    blk = jnp.arange(nb)[:, None, None] * DIL_QB
    valid = (dist >= 0) & (dist <= reach) & (blk + kj >= 0)
    p, lse = _masked_softmax(s, valid[None, None, :, None])
    o = jnp.einsum('brnhqk,brnkhd->brnqhd', p, vv)
    o = o.reshape(bsz, dil, m_len, nh, dh).transpose(0, 2, 1, 3, 4).reshape(bsz, s_pad, nh, dh)
    lse = lse.transpose(0, 1, 2, 4, 3).reshape(bsz, dil, m_len, nh).transpose(0, 2, 1, 3).reshape(bsz, s_pad, nh)
    return o[:, :s_len], lse[:, :s_len]


def _dilated_decode(q, k, v, buf, window, dil):
    t_len = q.shape[1]
    w_len = buf.shape[1]
    kc = jnp.concatenate([buf[:, :, 0], k.astype(buf.dtype)], axis=1)
    vc = jnp.concatenate([buf[:, :, 1], v.astype(buf.dtype)], axis=1)
    n_keys = window // dil + 1
    idx = w_len + jnp.arange(t_len)[:, None] - dil * jnp.arange(n_keys)[None, :]
    valid = idx >= 0
    idx = jnp.maximum(idx, 0)
    kg = kc[:, idx].astype(F32)
    vg = vc[:, idx].astype(F32)
    s = jnp.einsum('bthd,btkhd->bthk', q, kg)
    p, lse = _masked_softmax(s, valid[None, :, None, :])
    o = jnp.einsum('bthk,btkhd->bthd', p, vg)
    new_buf = jnp.stack([kc, vc], axis=2)[:, t_len:]
    return o, lse, new_buf


def _dilated(q_c, k_c, v_c, bufs):
    bsz, t_len, _ = q_c.shape
    shp = (bsz, t_len, DIL_HEADS, DIL_DH)
    q = q_c.astype(F32).reshape(shp) * (DIL_DH ** -0.5)
    k = k_c.reshape(shp)
    v = v_c.reshape(shp)
    outs, lses, new_bufs = [], [], []
    for g, (window, dil) in enumerate(DIL_GROUPS):
        sl = slice(g * DIL_HPG, (g + 1) * DIL_HPG)
        if bufs is None:
            o, lse = _dilated_prompt(q[:, :, sl], k[:, :, sl], v[:, :, sl], window, dil)
            kv = jnp.stack([k[:, :, sl], v[:, :, sl]], axis=2)
            new_bufs.append(kv[:, t_len - min(window, t_len):])
        else:
            o, lse, nbuf = _dilated_decode(q[:, :, sl], k[:, :, sl], v[:, :, sl], bufs[g], window, dil)
            new_bufs.append(nbuf)
        outs.append(o)
        lses.append(lse)
    w = jax.nn.softmax(jnp.stack(lses, axis=0), axis=0)
    o = jnp.sum(w[..., None] * jnp.stack(outs, axis=0), axis=0)
    return o.reshape(bsz, t_len, DIL_OUT).astype(q_c.dtype), tuple(new_bufs)


def _layer(x, pos0, conv_st, lru_st, hg_st, bufs, lb, lw):
    (n_mix_pre, n_mix_post, n_mlp_pre, n_mlp_post, w_in, b_gate, conv_w, conv_b, lru_wr, lru_br,
     lru_wi, lru_bi, lru_lambda, hg_norm, w_br_a, w_br_b, w_br_c, w_out, w_up, w_down) = lw
    bsz, t_len, _ = x.shape
    h = _rmsnorm(x, n_mix_pre)
    z = h @ w_in
    cuts, acc = [], 0
    for size in SPLIT_SIZES[:-1]:
        acc += size
        cuts.append(acc)
    u_a, q_b, f_b, i_b, g_b, q_c, k_c, v_c, gate_pre = jnp.split(z, cuts, axis=-1)
    y_a, conv_new, lru_new = _rglru(u_a, conv_st, lru_st, pos0, conv_w, conv_b, lru_wr, lru_br, lru_wi, lru_bi, lru_lambda)
    y_b, hg_new = _hgrn2(q_b, f_b, i_b, g_b, hg_st, lb, hg_norm)
    y_c, bufs_new = _dilated(q_c, k_c, v_c, bufs)
    gates = jax.nn.sigmoid((gate_pre + b_gate).astype(F32)).astype(x.dtype).reshape(bsz, t_len, N_BRANCH, D_MODEL)
    mixed = gates[:, :, 0] * (y_a @ w_br_a) + gates[:, :, 1] * (y_b @ w_br_b) + gates[:, :, 2] * (y_c @ w_br_c)
    x = x + _rmsnorm(mixed @ w_out, n_mix_post)
    hm = _rmsnorm(x, n_mlp_pre)
    x = x + _rmsnorm(jnp.square(jax.nn.relu(hm @ w_up)) @ w_down, n_mlp_post)
    return x, conv_new, lru_new, hg_new, bufs_new


def setup_inputs(seed: int = 0) -> dict:
    key = jax.random.key(seed)
    keys = jax.random.split(key, 40)
    counter = [0]

    def nxt():
        counter[0] += 1
        return keys[counter[0] - 1]

    def nrm(shape, scale):
        return jax.random.normal(nxt(), shape, jnp.float32) * scale

    def gain(shape):
        return 1.0 + nrm(shape, 0.02)

    wc = [min(w, PAST_LEN) for w, _ in DIL_GROUPS]
    x_prompt = nrm((BATCH, SEQ, D_MODEL), 1.0)
    x_sample = nrm((DEC_BATCH, DEC_SEQ, D_MODEL), 1.0)
    state_conv = nrm((DEPTH, DEC_BATCH, CONV_W - 1, LRU_WIDTH), 1.0)
    state_lru = nrm((DEPTH, DEC_BATCH, LRU_WIDTH), 0.5)
    state_hgrn = nrm((DEPTH, DEC_BATCH, HG_HEADS, HG_DK, HG_DV), 0.3)
    cache_win128 = nrm((DEPTH, DEC_BATCH, wc[0], 2, DIL_HPG, DIL_DH), 1.0)
    cache_win512 = nrm((DEPTH, DEC_BATCH, wc[1], 2, DIL_HPG, DIL_DH), 1.0)
    cache_win2048 = nrm((DEPTH, DEC_BATCH, wc[2], 2, DIL_HPG, DIL_DH), 1.0)
    norm_mix_pre = gain((DEPTH, D_MODEL))
    norm_mix_post = gain((DEPTH, D_MODEL))
    norm_mlp_pre = gain((DEPTH, D_MODEL))
    norm_mlp_post = gain((DEPTH, D_MODEL))
    w_in = nrm((DEPTH, D_MODEL, N_IN), D_MODEL ** -0.5)
    b_gate = nrm((DEPTH, N_BRANCH * D_MODEL), 0.01)
    conv_w = nrm((DEPTH, CONV_W, LRU_WIDTH), CONV_W ** -0.5)
    conv_b = nrm((DEPTH, LRU_WIDTH), 0.01)
    lru_wr = nrm((DEPTH, LRU_BLOCKS, LRU_BD, LRU_BD), LRU_BD ** -0.5)
    lru_br = nrm((DEPTH, LRU_WIDTH), 0.01)
    lru_wi = nrm((DEPTH, LRU_BLOCKS, LRU_BD, LRU_BD), LRU_BD ** -0.5)
    lru_bi = nrm((DEPTH, LRU_WIDTH), 0.01)
    a0 = jax.random.uniform(nxt(), (DEPTH, LRU_WIDTH), jnp.float32, 0.9, 0.999)
    s_a = a0 ** (1.0 / LRU_C)
    lru_lambda = jnp.log(s_a) - jnp.log1p(-s_a)
    hgrn_lb_raw = nrm((DEPTH, HG_WIDTH), 0.5)
    hgrn_norm = gain((DEPTH, HG_DV))
    w_br_a = nrm((DEPTH, LRU_WIDTH, D_MODEL), LRU_WIDTH ** -0.5)
    w_br_b = nrm((DEPTH, HG_WIDTH, D_MODEL), HG_WIDTH ** -0.5)
    w_br_c = nrm((DEPTH, DIL_OUT, D_MODEL), DIL_OUT ** -0.5)
    w_out = nrm((DEPTH, D_MODEL, D_MODEL), D_MODEL ** -0.5)
    w_mlp_up = nrm((DEPTH, D_MODEL, D_FF), D_MODEL ** -0.5)
    w_mlp_down = nrm((DEPTH, D_FF, D_MODEL), D_FF ** -0.5)
    return {'x_prompt': x_prompt, 'x_sample': x_sample, 'state_conv': state_conv, 'state_lru': state_lru,
            'state_hgrn': state_hgrn, 'cache_win128': cache_win128, 'cache_win512': cache_win512,
            'cache_win2048': cache_win2048, 'norm_mix_pre': norm_mix_pre, 'norm_mix_post': norm_mix_post,
            'norm_mlp_pre': norm_mlp_pre, 'norm_mlp_post': norm_mlp_post, 'w_in': w_in, 'b_gate': b_gate,
            'conv_w': conv_w, 'conv_b': conv_b, 'lru_wr': lru_wr, 'lru_br': lru_br, 'lru_wi': lru_wi,
            'lru_bi': lru_bi, 'lru_lambda': lru_lambda, 'hgrn_lb_raw': hgrn_lb_raw, 'hgrn_norm': hgrn_norm,
            'w_br_a': w_br_a, 'w_br_b': w_br_b, 'w_br_c': w_br_c, 'w_out': w_out,
            'w_mlp_up': w_mlp_up, 'w_mlp_down': w_mlp_down}


def reference(x_prompt, x_sample, state_conv, state_lru, state_hgrn, cache_win128, cache_win512, cache_win2048,
              norm_mix_pre, norm_mix_post, norm_mlp_pre, norm_mlp_post, w_in, b_gate, conv_w, conv_b,
              lru_wr, lru_br, lru_wi, lru_bi, lru_lambda, hgrn_lb_raw, hgrn_norm, w_br_a, w_br_b, w_br_c,
              w_out, w_mlp_up, w_mlp_down):
    lb_soft = jax.nn.softmax(hgrn_lb_raw.astype(F32), axis=0)
    lb_all = jnp.cumsum(lb_soft, axis=0) - lb_soft[0]
    bp = x_prompt.shape[0]
    dt = x_prompt.dtype
    yp, ys = x_prompt, x_sample
    new_p = [[] for _ in range(6)]
    new_s = [[] for _ in range(6)]
    for l in range(DEPTH):
        lw = (norm_mix_pre[l], norm_mix_post[l], norm_mlp_pre[l], norm_mlp_post[l], w_in[l], b_gate[l],
              conv_w[l], conv_b[l], lru_wr[l], lru_br[l], lru_wi[l], lru_bi[l], lru_lambda[l], hgrn_norm[l],
              w_br_a[l], w_br_b[l], w_br_c[l], w_out[l], w_mlp_up[l], w_mlp_down[l])
        yp, c_p, h_p, s_p, b_p = _layer(
            yp, 0,
            jnp.zeros((bp, CONV_W - 1, LRU_WIDTH), dt),
            jnp.zeros((bp, LRU_WIDTH), dt),
            jnp.zeros((bp, HG_HEADS, HG_DK, HG_DV), dt),
            None, lb_all[l], lw)
        ys, c_s, h_s, s_s, b_s = _layer(
            ys, PAST_LEN, state_conv[l], state_lru[l], state_hgrn[l],
            (cache_win128[l], cache_win512[l], cache_win2048[l]), lb_all[l], lw)
        for lst, val in zip(new_p, (c_p, h_p, s_p) + b_p):
            lst.append(val)
        for lst, val in zip(new_s, (c_s, h_s, s_s) + b_s):
            lst.append(val)
    conv_p, lru_p, hgrn_p, win128_p, win512_p, win2048_p = [jnp.stack(v, axis=0) for v in new_p]
    conv_s, lru_s, hgrn_s, win128_s, win512_s, win2048_s = [jnp.stack(v, axis=0) for v in new_s]
    return (yp, ys, conv_p, conv_s, lru_p, lru_s, hgrn_p, hgrn_s,
            win128_p, win128_s, win512_p, win512_s, win2048_p, win2048_s)
```

```python
import numpy as np
from contextlib import ExitStack
import concourse.bass as bass
import concourse.mybir as mybir
from concourse.bass_utils import run_bass_kernel_spmd

F32 = mybir.dt.float32
BF16 = mybir.dt.bfloat16
ALU = mybir.AluOpType
AF = mybir.ActivationFunctionType
AX = mybir.AxisListType

NCORE = 8
TP = 2048
NS = 4
TS = 8
T = TP + NS * TS
D = 1024
KT = 8
NIN = 7936
EPS = 1e-6
BLOCKS = [(0, 512), (512, 512), (1024, 512), (1536, 512), (2048, 32)]
DIL = (1, 4, 16)
WIN = (128, 512, 2048)
HAL = (128, 512, 2048)
NB = (16, 4, 1)
UW = 3 + TP + NS * 11
C_MASK = 0
C_CM = 256
C_SM = 320
C_ID = 392
NCS = 520
C_RM = 520
NCONST = 2600
V_NPRE, V_NPOST, V_MPRE, V_MPOST, V_BG, V_CW, V_CB, V_BR, V_BI, V_LAM, V_LB, V_HN = 0, 8, 16, 24, 32, 56, 72, 76, 80, 84, 88, 96
NVEC = 97


import os
STOPAT = float(os.environ.get("STOPAT", "99"))
NOCC = os.environ.get("NOCC", "0") == "1"
SKIP = os.environ.get("SKIP", "").split(",")
DBGH = os.environ.get("DBGH", "0") == "1"
_LAST = [None]


class StopBuild(Exception):
    pass


DEAD = [False]


def stop(n):
    if STOPAT == n:
        DEAD[0] = True


class Buf:
    def __init__(self, name=""):
        self.name = name
        self.w = None
        self.r = {}


class LEng:
    def __init__(self, name, stream, sem, inc, inorder):
        self.name, self.stream, self.sem, self.inc, self.inorder = name, stream, sem, inc, inorder
        self.count = 0


class Stream:
    def __init__(self, name):
        self.name = name
        self.ops = []
        self.seen = {}


def emit(le, fn, reads=(), writes=()):
    if DEAD[0]:
        return
    st = le.stream
    deps = {}

    def need(o, cnt):
        if o is le and le.inorder and le.name == "pe":
            return
        if st.seen.get(o, 0) >= cnt:
            return
        deps[o] = max(deps.get(o, 0), cnt)

    for b in reads:
        if b.w:
            need(*b.w)
    for b in writes:
        if b.w:
            need(*b.w)
        for o, cnt in b.r.items():
            need(o, cnt)
    if not le.inorder and le.count > 0:
        need(le, le.count)
    for o, cnt in deps.items():
        st.ops.append(("w", o.sem, cnt))
        st.seen[o] = cnt
    le.count += le.inc
    st.ops.append(("o", fn, le.sem, le.inc))
    for b in reads:
        b.r[le] = le.count
    for b in writes:
        b.w = (le, le.count)
        b.r = {}


def make_helpers(nc, es):
    def sem(name):
        return es.enter_context(nc.semaphore(name))

    S = {n: Stream(n) for n in ("sync", "act", "dve", "pool", "pe")}
    PE = LEng("pe", S["pe"], sem("s_pe"), 1, True)
    ACT = LEng("act", S["act"], sem("s_act"), 1, True)
    DVE = LEng("dve", S["dve"], sem("s_dve"), 1, True)
    DQ = [LEng("dq%d" % i, S["sync"], sem("s_dq%d" % i), 16, False) for i in range(6)]
    GQ = [LEng("gq%d" % i, S["pool"], sem("s_gq%d" % i), 16, False) for i in range(3)]
    CC = LEng("cc", S["pool"], sem("s_cc"), 1, False)
    rr = {"d": 0, "g": 0}

    def dma(out, in_, reads, writes):
        le = DQ[rr["d"] % len(DQ)]; rr["d"] += 1
        emit(le, lambda e: e.dma_start(out=out, in_=in_), reads, writes)

    def gdma(out, in_, reads, writes):
        le = GQ[rr["g"] % len(GQ)]; rr["g"] += 1
        emit(le, lambda e: e.dma_start(out=out, in_=in_), reads, writes)

    def allgather(snd, rcv, bs, br):
        sa = snd if hasattr(snd, "opt") else snd.ap()
        ra = rcv if hasattr(rcv, "opt") else rcv.ap()
        if NOCC:
            for c_ in range(NCORE):
                dma(ra[c_ * 128:(c_ + 1) * 128, :], sa[:, :], [bs], [br])
            return
        emit(CC, lambda e: e.collective_compute("AllGather", ALU.bypass, replica_groups=[list(range(NCORE))],
                                                 ins=[sa.opt()], outs=[ra.opt()]), [bs], [br])

    def mm(out, lhsT, rhs, start, stop, reads, writes):
        emit(PE, lambda e: e.matmul(out, lhsT=lhsT, rhs=rhs, start=start, stop=stop), reads, writes)

    def tr(out, in_, ident, reads, writes):
        emit(PE, lambda e: e.transpose(out, in_, ident), reads, writes)

    def act(out, in_, func, reads, writes, bias=None, scale=None):
        kw = {}
        if bias is not None:
            kw["bias"] = bias
        if scale is not None:
            kw["scale"] = scale
        emit(ACT, lambda e: e.activation(out=out, in_=in_, func=func, **kw), reads, writes)

    def acopy(out, in_, reads, writes):
        emit(ACT, lambda e: e.copy(out=out, in_=in_), reads, writes)

    def vcopy(out, in_, reads, writes):
        emit(DVE, lambda e: e.tensor_copy(out=out, in_=in_), reads, writes)

    def tt(out, a, b, op, reads, writes):
        emit(DVE, lambda e: e.tensor_tensor(out=out, in0=a, in1=b, op=op), reads, writes)

    def ts(out, a, s1, s2, op0, op1, reads, writes):
        emit(DVE, lambda e: e.tensor_scalar(out=out, in0=a, scalar1=s1, scalar2=s2, op0=op0, op1=op1), reads, writes)

    def stt(out, a, s, b, op0, op1, reads, writes):
        emit(DVE, lambda e: e.scalar_tensor_tensor(out=out, in0=a, scalar=s, in1=b, op0=op0, op1=op1), reads, writes)

    def recip(out, in_, reads, writes):
        emit(DVE, lambda e: e.reciprocal(out=out, in_=in_), reads, writes)

    def scan(out, d0, d1, init, reads, writes):
        emit(DVE, lambda e: e.tensor_tensor_scan(out=out, data0=d0, data1=d1, initial=init, op0=ALU.mult, op1=ALU.add), reads, writes)

    def memset(t, val, writes):
        emit(DVE, lambda e: e.memset(t, val), [], writes)

    def rsum(out, in_, reads, writes):
        emit(DVE, lambda e: e.reduce_sum(out=out, in_=in_, axis=AX.X), reads, writes)

    nmc = [0]

    freed = {}

    def sb(stack, name, shape, dt=F32):
        nmc[0] += 1
        name = "%s_%d" % (name, nmc[0])
        t = stack.enter_context(nc.sbuf_tensor(name, list(shape), dt))
        b = Buf(name)
        b.r = dict(freed)

        def _on_free(b=b):
            if b.w:
                freed[b.w[0]] = max(freed.get(b.w[0], 0), b.w[1])
            for o_, c_ in b.r.items():
                freed[o_] = max(freed.get(o_, 0), c_)
        stack.callback(_on_free)
        return t, b


    return dict(locals())


def finalize(nc, H):
    S, DQ, GQ, CC, PE, ACT, DVE = H['S'], H['DQ'], H['GQ'], H['CC'], H['PE'], H['ACT'], H['DVE']
    fin = S["sync"]
    for le in DQ + GQ + [CC, PE, ACT, DVE]:
        if le.count > 0 and fin.seen.get(le, 0) < le.count:
            fin.ops.append(("w", le.sem, le.count))

    def run(st, e):
        for op in st.ops:
            if op[0] == "w":
                e.wait_ge(op[1], op[2])
            else:
                ins = op[1](e)
                ins.then_inc(op[2], op[3])

    with nc.Block() as block:
        @block.sync
        def _(e):
            run(S["sync"], e)

        @block.scalar
        def _(e):
            run(S["act"], e)

        @block.vector
        def _(e):
            run(S["dve"], e)

        @block.gpsimd
        def _(e):
            run(S["pool"], e)

        @block.tensor
        def _(e):
            run(S["pe"], e)


def build_nc():
    nc = bass.Bass("TRN2", target_bir_lowering=False)

    def din(name, shape, dt=F32):
        return nc.dram_tensor(name, list(shape), dt, kind="ExternalInput").ap()

    def dout(name, shape, dt=F32):
        return nc.dram_tensor(name, list(shape), dt, kind="ExternalOutput").ap()

    xp = din("xp", [TP, D]); xs = din("xs", [NS * TS, D])
    st_conv = din("st_conv", [2, NS, 512, 3]); st_lru = din("st_lru", [2, 512, NS])
    st_hg = din("st_hg", [2, NS, 4, 128, 128])
    cache = [din("cache%d" % g, [2, NS, WIN[g], 512]) for g in range(3)]
    w_in = din("w_in", [2, D, NIN]); w_bra = din("w_bra", [2, 512, D]); w_brb = din("w_brb", [2, 512, D])
    w_brc = din("w_brc", [2, 256, D]); w_out = din("w_out", [2, D, D]); w_up = din("w_up", [2, D, 4096])
    w_down = din("w_down", [2, 4096, D]); lru_wr = din("lru_wr", [2, 8, 64, 64]); lru_wi = din("lru_wi", [2, 8, 64, 64])
    vecs = din("vecs", [2, 128, NVEC]); consts = din("consts", [128, NCONST]); meta = din("meta", [128, 24])

    o_y = dout("o_y", [T, D])
    o_conv_p = dout("o_conv_p", [2, 512, 3]); o_conv_s = dout("o_conv_s", [2, NS, 512, 3])
    o_lru_p = dout("o_lru_p", [2, 128, 4]); o_lru_s = dout("o_lru_s", [2, 128, 4, NS])
    o_hg_p = dout("o_hg_p", [2, 4, 128, 128]); o_hg_s = dout("o_hg_s", [2, NS, 4, 128, 128])
    o_win_p = [dout("o_win_p%d" % g, [2, WIN[g], 512]) for g in range(3)]
    o_win_s = [dout("o_win_s%d" % g, [2, NS, WIN[g], 512]) for g in range(3)]

    xT_d = nc.dram_tensor("xT_d", [D, T], F32).ap()
    o_dbg = dout("o_dbg", [128, 8, T]) if DBGH else None
    NCH = [(2 * HAL[g] + 256 * DIL[g]) // 256 for g in range(3)]
    sendK = [[nc.dram_tensor("sendK%d_%d" % (g, i), [128, 256], BF16) for i in range(NCH[g])] for g in range(3)]
    recvK = [[nc.dram_tensor("recvK%d_%d" % (g, i), [NCORE * 128, 256], BF16) for i in range(NCH[g])] for g in range(3)]
    sendU = nc.dram_tensor("sendU", [128, 12], F32); recvU = nc.dram_tensor("recvU", [NCORE * 128, 12], F32)
    sendS = nc.dram_tensor("sendS", [128, 8], F32); recvS = nc.dram_tensor("recvS", [NCORE * 128, 8], F32)
    sendH = [nc.dram_tensor("sendH%d" % i, [128, 128], F32) for i in range(5)]
    recvH = [nc.dram_tensor("recvH%d" % i, [NCORE * 128, 128], F32) for i in range(5)]
    b_xT = Buf("xT_d"); b_sendK, b_recvK, b_sendU, b_recvU = Buf(), Buf(), Buf(), Buf()
    b_sendS, b_recvS, b_sendH, b_recvH = Buf(), Buf(), Buf(), Buf()
    b_out = Buf("outs")

    es = ExitStack()
    with es:
        H = make_helpers(nc, es)
        S, PE, ACT, DVE, DQ, GQ, CC, dma, gdma, allgather, mm, tr, act, acopy, vcopy, tt, ts, stt, recip, scan, memset, rsum, sb = H['S'], H['PE'], H['ACT'], H['DVE'], H['DQ'], H['GQ'], H['CC'], H['dma'], H['gdma'], H['allgather'], H['mm'], H['tr'], H['act'], H['acopy'], H['vcopy'], H['tt'], H['ts'], H['stt'], H['recip'], H['scan'], H['memset'], H['rsum'], H['sb']

        cst, b_cst = sb(es, "cst", [128, NCS])
        met, b_met = sb(es, "met", [128, 24])
        vec, b_vec = sb(es, "vec", [128, 2, NVEC])
        onesf, b_onesf = sb(es, "onesf", [128, 128])
        onesb, b_onesb = sb(es, "onesb", [128, 64], BF16)
        identb, b_identb = sb(es, "identb", [128, 128], BF16)
        mask4, b_mask4 = sb(es, "mask4", [128, 1024], BF16)
        mask4f, b_mask4f = sb(es, "mask4f", [128, 1024], BF16)
        smask, b_smask = sb(es, "smask", [128, 9, 32], BF16)
        epsc, b_epsc = sb(es, "epsc", [128, 2])
        lbv, b_lbv = sb(es, "lbv", [128, 2, 3, 4])
        hT, b_hT = sb(es, "hT", [128, KT, T], BF16)
        psA = es.enter_context(nc.psum_tensor("psA", [128, 1024], F32)); b_psA = Buf("psA")
        psD = [es.enter_context(nc.psum_tensor("psD%d" % i, [128, 512], F32)) for i in range(4)]
        b_psD = [Buf("psD%d" % i) for i in range(4)]
        psO = es.enter_context(nc.psum_tensor("psO", [128, 512], F32)); b_psO = Buf("psO")
        psT = es.enter_context(nc.psum_tensor("psT", [128, 512], BF16)); b_psT = Buf("psT")
        identf = cst[:, C_ID:C_ID + 128]
        onehot = met[:, 0:8]; pmask = met[:, 8:16]; halo_valid = met[:, 16:17]; first_flag = met[:, 17:18]

        dma(cst[:], consts[:, 0:NCS], [], [b_cst])
        dma(met[:], meta[:, :], [], [b_met])
        dma(vec[:], vecs.rearrange("l p n -> p l n"), [], [b_vec])
        memset(onesf[:], 1.0, [b_onesf]); memset(onesb[:], 1.0, [b_onesb])
        memset(epsc[:, 0:1], EPS, [b_epsc]); memset(epsc[:, 1:2], 1.0, [b_epsc])
        vcopy(identb[:], identf, [b_cst], [b_identb])
        for j in range(4):
            vcopy(mask4[:, j * 256:(j + 1) * 256], cst[:, C_MASK:C_MASK + 256], [b_cst], [b_mask4])
            vcopy(mask4f[:, j * 256 + 128:(j + 1) * 256], cst[:, C_MASK + 128:C_MASK + 256], [b_cst], [b_mask4f])
            ts(mask4f[:, j * 256:j * 256 + 128], cst[:, C_MASK:C_MASK + 128], halo_valid, None, ALU.mult, ALU.bypass, [b_cst, b_met], [b_mask4f])
            for i in range(9):
                vcopy(smask[:, i, j * 8:(j + 1) * 8], cst[:, C_SM + i * 8:C_SM + i * 8 + 8], [b_cst], [b_smask])
        omp, b_omp = sb(es, "omp", [128, 8])
        ts(omp[:], pmask, -1.0, 1.0, ALU.mult, ALU.add, [b_met], [b_omp])
        memset(lbv[:, 0, 0, :], 0.0, [b_lbv])
        tt(lbv[:, 1, 1, :], vec[:, 1, V_LB + 4:V_LB + 8], vec[:, 0, V_LB:V_LB + 4], ALU.subtract, [b_vec], [b_lbv])
        act(lbv[:, 1, 0, :], lbv[:, 1, 1, :], AF.Sigmoid, [b_lbv], [b_lbv])
        for l in range(2):
            ts(lbv[:, l, 1, :], lbv[:, l, 0, :], -1.0, 1.0, ALU.mult, ALU.add, [b_lbv], [b_lbv])
            ts(lbv[:, l, 2, :], lbv[:, l, 1, :], -1.0, None, ALU.mult, ALU.bypass, [b_lbv], [b_lbv])

        wcnt = [0]

        def load_w(stack_tiles, src_ap, kt, ncols):
            t, b = stack_tiles[wcnt[0] % len(stack_tiles)]; wcnt[0] += 1
            gdma(t[:, 0:kt, 0:ncols], src_ap.rearrange("(k p) c -> p k c", p=128), [], [b])
            return t, b

        wt = [sb(es, "wt%d" % i, [128, KT, 512], BF16) for i in range(2)]
        dcnt = [0]

        def psd():
            i = dcnt[0] % 4; dcnt[0] += 1
            return psD[i], b_psD[i]

        evc = [0]

        def evac_copy(out, in_, reads, writes):
            evc[0] += 1
            if evc[0] % 2:
                acopy(out, in_, reads, writes)
            else:
                vcopy(out, in_, reads, writes)

        def proj_fm(src, ncols, rhsT, b_rhs, kt, evac, blocks=BLOCKS):
            w, bw = load_w(wt, src, kt, ncols)
            for j in range(ncols // 128):
                for (c0, n) in blocks:
                    ps, bps = psd()
                    for k in range(kt):
                        mm(ps[:, 0:n], w[:, k, j * 128:(j + 1) * 128], rhsT[:, k, c0:c0 + n], k == 0, k == kt - 1, [bw, b_rhs], [bps])
                    evac(j, c0, n, ps, bps)

        def rms_rinv(stack, src, b_src, kt, c0, n, tagd, inv_d):
            sq, b_sq = tagd["sq"]
            rv, b_rv = tagd["rv"]
            ps, bps = psd()
            for k in range(kt):
                act(sq[:, 0:n], src[:, k, c0:c0 + n], AF.Square, [b_src], [b_sq])
                mm(ps[:, 0:n], onesf[:], sq[:, 0:n], k == 0, k == kt - 1, [b_onesf, b_sq], [bps])
            act(rv[:, 0:n], ps[:, 0:n], AF.Sqrt, [bps, b_epsc], [b_rv], bias=epsc[:, 0:1], scale=inv_d)
            recip(rv[:, 0:n], rv[:, 0:n], [b_rv], [b_rv])
            return rv, b_rv

        with ExitStack() as ph:
            xt, b_xt = sb(ph, "xt", [128, D]); xf, b_xf = sb(ph, "xf", [128, KT, 128])
            for ti in range(TP // 128 + 1):
                nt = 128 if ti < TP // 128 else NS * TS
                src = xp[ti * 128:(ti + 1) * 128, :] if ti < TP // 128 else xs[:, :]
                dma(xt[0:nt, :], src, [], [b_xt])
                for k in range(KT):
                    tr(psA[:, k * 128:k * 128 + nt], xt[0:nt, k * 128:(k + 1) * 128], identf[0:nt, 0:nt], [b_xt, b_cst], [b_psA])
                evac_copy(xf[:, :, 0:nt], psA[:, :].rearrange("p (k t) -> p k t", k=KT)[:, :, 0:nt], [b_psA], [b_xf])
                dma(xT_d.rearrange("(k p) t -> p k t", p=128)[:, :, ti * 128:ti * 128 + nt], xf[:, :, 0:nt], [b_xf], [b_xT])

        xTv = xT_d.rearrange("(k p) t -> p k t", p=128)

        try:
          stop(1)
          for l in range(2):
              V = lambda c0, n=1: vec[:, l, c0:c0 + n]
              with ExitStack() as lay:
                  with ExitStack() as ph:
                      xb, b_xb = sb(ph, "xb", [128, KT, 512]); sq = sb(ph, "sq", [128, 512]); rv = sb(ph, "rv", [128, 512])
                      for (c0, n) in BLOCKS:
                          dma(xb[:, :, 0:n], xTv[:, :, c0:c0 + n], [b_xT], [b_xb])
                          r, br_ = rms_rinv(ph, xb, b_xb, KT, 0, n, {"sq": sq, "rv": rv}, 1.0 / D)
                          for k in range(KT):
                              stt(hT[:, k, c0:c0 + n], xb[:, k, 0:n], V(V_NPRE + k), r[:, 0:n], ALU.mult, ALU.mult, [b_xb, b_vec, br_], [b_hT])
                  stop(2)
                  ycT, b_yc = sb(lay, "ycT", [128, 2, T], BF16)
                  with ExitStack() as ph:
                      acco, b_acco = sb(ph, "acco", [128, 2, T]); accl, b_accl = sb(ph, "accl", [128, 2, T])
                      kvf, b_kvf = sb(ph, "kvf", [128, 512])
                      pexp, b_pexp = sb(ph, "pexp", [128, 1024], BF16); pm, b_pm = sb(ph, "pm", [128, 1024], BF16)
                      ckv, b_ckv = sb(ph, "ckv", [128, 512]); ckb, b_ckb = sb(ph, "ckb", [128, 512], BF16)
                      kTs, b_kTs = sb(ph, "kTs", [128, 2, 128], BF16)
                      spe, b_spe = sb(ph, "spe", [128, 32], BF16); spm, b_spm = sb(ph, "spm", [128, 32], BF16)
                      memset(acco[:], 0.0, [b_acco]); memset(accl[:], 0.0, [b_accl])
                      for g in range(3):
                          d, nb = DIL[g], NB[g]
                          LKg = 2 * HAL[g] + 256 * d
                          with ExitStack() as gp:
                              kT, bk = sb(gp, "kT", [128, 2, T], BF16); qT, bq = sb(gp, "qT", [128, 2, T], BF16)
                              vt, bv = sb(gp, "vtk", [128, 16, 256], BF16); vnew, b_vnew = sb(gp, "vnew", [8, NS, 256], BF16)
                              halo, b_halo = sb(gp, "halo", [128, LKg], BF16)
                              tmpK, b_tmpK = sb(gp, "tmpK", [128, NCORE, 256], BF16)
                              memset(halo[:], 0.0, [b_halo])

                              def ev_k(j, c0, n, ps, bps):
                                  evac_copy(kT[:, j, c0:c0 + n], ps[:, 0:n], [bps], [bk])

                              def ev_q(j, c0, n, ps, bps):
                                  emit(ACT, lambda e: e.mul(out=qT[:, j, c0:c0 + n], in_=ps[:, 0:n], mul=0.125), [bps], [bq])
                              proj_fm(w_in[l, :, 3328 + g * 256:3328 + (g + 1) * 256], 256, hT, b_hT, KT, ev_k)
                              proj_fm(w_in[l, :, 2560 + g * 256:2560 + (g + 1) * 256], 256, hT, b_hT, KT, ev_q)
                              if l == 0 and g == 0:
                                  stop(2.1)
                              w, bw = wt[wcnt[0] % 2]; wcnt[0] += 1
                              gdma(w[:, :, 0:256], w_in[l, :, 3328 + g * 256:3328 + (g + 1) * 256].rearrange("(k p) c -> p k c", p=128), [], [bw])
                              gdma(w[:, :, 256:512], w_in[l, :, 4096 + g * 256:4096 + (g + 1) * 256].rearrange("(k p) c -> p k c", p=128), [], [bw])
                              for r in range(d):
                                  for n_ in range(nb):
                                      ps, bps = psd()
                                      t0 = r + n_ * 128 * d
                                      for k in range(KT):
                                          if "mm" not in SKIP:
                                              mm(ps[:, :], hT[:, k, t0:t0 + 128 * d:d], w[:, k, :], k == 0, k == KT - 1, [b_hT, bw], [bps])
                                      if "vt" not in SKIP:
                                          vcopy(vt[:, r * nb + n_, :], ps[:, 256:512], [bps], [bv])
                                      if n_ == nb - 1:
                                          if "kvf" not in SKIP:
                                              vcopy(kvf[:], ps[:, :], [bps], [b_kvf])
                                          if "dma" not in SKIP:
                                              dma(o_win_p[g][l, r:WIN[g]:d, :], kvf[:], [b_kvf], [b_out])
                              if l == 0 and g == 0:
                                  stop(2.15)
                              for s in range(NS):
                                  ps, bps = psd()
                                  for k in range(KT):
                                      mm(ps[0:8, :], hT[:, k, TP + 8 * s:TP + 8 * s + 8], w[:, k, :], k == 0, k == KT - 1, [b_hT, bw], [bps])
                                  vcopy(vnew[:, s, :], ps[0:8, 256:512], [bps], [b_vnew])
                                  vcopy(kvf[0:8, :], ps[0:8, :], [bps], [b_kvf])
                                  dma(o_win_s[g][l, s, WIN[g] - 8:WIN[g], :], kvf[0:8, :], [b_kvf], [b_out])
                                  dma(o_win_s[g][l, s, 0:WIN[g] - 8, :], cache[g][l, s, 8:WIN[g], :], [], [b_out])
                              if l == 0 and g == 0:
                                  stop(2.2)
                              nkc = max(1, 2 * HAL[g] // 256)
                              if g == 0:
                                  dma(sendK[g][0].ap().rearrange("p (h t) -> p h t", h=2), kT[:, :, TP - HAL[g]:TP], [bk], [b_sendK])
                              else:
                                  A_ = HAL[g] // 256
                                  for hh in range(2):
                                      for a_ in range(A_):
                                          dma(sendK[g][hh * A_ + a_].ap(), kT[:, hh, TP - HAL[g] + a_ * 256:TP - HAL[g] + (a_ + 1) * 256], [bk], [b_sendK])
                              for r_ in range(d):
                                  dma(sendK[g][nkc + r_].ap(), vt[:, r_ * nb + nb - 1, :], [bv], [b_sendK])
                              for ch in range(NCH[g]):
                                  allgather(sendK[g][ch], recvK[g][ch], b_sendK, b_recvK)
                              for ch in range(NCH[g]):
                                  dma(tmpK[:], recvK[g][ch].ap().rearrange("(c p) n -> p c n", p=128), [b_recvK], [b_tmpK])
                                  for c in range(NCORE):
                                      stt(halo[:, ch * 256:(ch + 1) * 256], tmpK[:, c, :], onehot[:, c:c + 1], halo[:, ch * 256:(ch + 1) * 256], ALU.mult, ALU.add,
                                          [b_tmpK, b_met, b_halo], [b_halo])
                              if l == 0 and g == 0:
                                  stop(2.3)
                              hk = halo[:, 0:2 * HAL[g]].rearrange("p (h t) -> p h t", h=2)
                              hv = halo[:, 2 * HAL[g]:LKg].rearrange("p (r c) -> p r c", c=256)
                              for r in range(d):
                                  for n_ in range(nb):
                                      t0 = r + n_ * 128 * d
                                      qs = slice(t0, t0 + 128 * d, d)
                                      for j in range(4):
                                          hp, e = j // 2, j % 2
                                          pr = slice(64 * e, 64 * e + 64)
                                          if n_ == 0:
                                              kprev = hk[pr, hp, r:HAL[g]:d]; rk = [b_halo]
                                          else:
                                              kprev = kT[pr, hp, t0 - 128 * d:t0:d]; rk = [bk]
                                          sc0 = e * 512 + hp * 256
                                          mm(psA[:, sc0:sc0 + 128], kprev, qT[pr, hp, qs], True, True, rk + [bq], [b_psA])
                                          mm(psA[:, sc0 + 128:sc0 + 256], kT[pr, hp, qs], qT[pr, hp, qs], True, True, [bk, bq], [b_psA])
                                      act(pexp[:], psA[:, :], AF.Exp, [b_psA], [b_pexp])
                                      mk, bmk = (mask4f, b_mask4f) if n_ == 0 else (mask4, b_mask4)
                                      tt(pm[:], pexp[:], mk[:], ALU.mult, [b_pexp, bmk], [b_pm])
                                      for j in range(4):
                                          hp, e = j // 2, j % 2
                                          po = slice(64 * e, 64 * e + 64)
                                          if n_ == 0:
                                              vprev = hv[:, r, j * 64:(j + 1) * 64]; rv_ = [b_halo]
                                          else:
                                              vprev = vt[:, r * nb + n_ - 1, j * 64:(j + 1) * 64]; rv_ = [bv]
                                          vcur = vt[:, r * nb + n_, j * 64:(j + 1) * 64]
                                          sc0 = e * 512 + hp * 256
                                          pp_ = pm[:, sc0:sc0 + 128]; pc = pm[:, sc0 + 128:sc0 + 256]
                                          mm(psO[po, hp * 128:(hp + 1) * 128], vprev, pp_, True, False, rv_ + [b_pm], [b_psO])
                                          mm(psO[po, hp * 128:(hp + 1) * 128], vcur, pc, False, True, [bv, b_pm], [b_psO])
                                          mm(psO[po, 256 + hp * 128:256 + (hp + 1) * 128], onesb[:, :], pp_, True, False, [b_onesb, b_pm], [b_psO])
                                          mm(psO[po, 256 + hp * 128:256 + (hp + 1) * 128], onesb[:, :], pc, False, True, [b_onesb, b_pm], [b_psO])
                                      tt(acco[:, :, qs], psO[:, 0:256].rearrange("p (h t) -> p h t", h=2), acco[:, :, qs], ALU.add, [b_psO, b_acco], [b_acco])
                                      tt(accl[:, :, qs], psO[:, 256:512].rearrange("p (h t) -> p h t", h=2), accl[:, :, qs], ALU.add, [b_psO, b_accl], [b_accl])
                              if l == 0 and g == 0:
                                  stop(2.4)
                              for s in range(NS):
                                  qc = slice(TP + 8 * s, TP + 8 * s + 8)
                                  ntile = WIN[g] // 128
                                  for ti in range(ntile + 1):
                                      new = ti == ntile
                                      nk = 8 if new else 128
                                      if not new:
                                          dma(ckv[:], cache[g][l, s, ti * 128:(ti + 1) * 128, :], [], [b_ckv])
                                          vcopy(ckb[:], ckv[:], [b_ckv], [b_ckb])
                                          for hp in range(2):
                                              tr(psT[:, hp * 128:(hp + 1) * 128], ckb[:, hp * 128:(hp + 1) * 128], identb[:], [b_ckb, b_identb], [b_psT])
                                          acopy(kTs[:], psT[:, 0:256].rearrange("p (h t) -> p h t", h=2), [b_psT], [b_kTs])
                                      psE = [psd(), psd()]
                                      for j in range(4):
                                          hp, e = j // 2, j % 2
                                          pr = slice(64 * e, 64 * e + 64)
                                          if new:
                                              kk_ = kT[pr, hp, qc]; rk = [bk]
                                          else:
                                              kk_ = kTs[pr, hp, :]; rk = [b_kTs]
                                          mm(psE[e][0][0:nk, hp * 8:(hp + 1) * 8], kk_, qT[pr, hp, qc], True, True, rk + [bq], [psE[e][1]])
                                      for e in range(2):
                                          act(spe[0:nk, e * 16:(e + 1) * 16], psE[e][0][0:nk, 0:16], AF.Exp, [psE[e][1]], [b_spe])
                                      mi = g * 3 + (2 if new else (0 if ti == 0 else 1))
                                      tt(spm[0:nk, :], spe[0:nk, :], smask[0:nk, mi, :], ALU.mult, [b_spe, b_smask], [b_spm])
                                      ps2, bps2 = psd()
                                      for j in range(4):
                                          hp, e = j // 2, j % 2
                                          po = slice(64 * e, 64 * e + 64)
                                          if new:
                                              vv = vnew[0:8, s, j * 64:(j + 1) * 64]; rv_ = [b_vnew]
                                          else:
                                              vv = ckb[:, 256 + j * 64:256 + (j + 1) * 64]; rv_ = [b_ckb]
                                          sj = e * 16 + hp * 8
                                          mm(ps2[po, hp * 8:(hp + 1) * 8], vv, spm[0:nk, sj:sj + 8], True, True, rv_ + [b_spm], [bps2])
                                          mm(ps2[po, 16 + hp * 8:16 + (hp + 1) * 8], onesb[0:nk, :], spm[0:nk, sj:sj + 8], True, True, [b_onesb, b_spm], [bps2])
                                      tt(acco[:, :, qc], ps2[:, 0:16].rearrange("p (h t) -> p h t", h=2), acco[:, :, qc], ALU.add, [bps2, b_acco], [b_acco])
                                      tt(accl[:, :, qc], ps2[:, 16:32].rearrange("p (h t) -> p h t", h=2), accl[:, :, qc], ALU.add, [bps2, b_accl], [b_accl])
                      recip(accl[:], accl[:], [b_accl], [b_accl])
                      tt(ycT[:], acco[:], accl[:], ALU.mult, [b_acco, b_accl], [b_yc])
                      stop(2.9)

                  stop(3)
                  yaT, b_ya = sb(lay, "yaT", [128, 4, T], BF16)
                  with ExitStack() as ph:
                      aa, b_aa = sb(ph, "aa", [128, 4, T]); bt, b_bt = sb(ph, "bt", [128, 4, T])
                      uT, b_uT = sb(ph, "uT", [128, UW]); utl, b_utl = sb(ph, "utl", [128, 12]); tmpU, b_tmpU = sb(ph, "tmpU", [128, 12])
                      xc, b_xc = sb(ph, "xc", [128, T]); xcb, b_xcb = sb(ph, "xcb", [128, T], BF16)
                      rg, b_rg = sb(ph, "rg", [128, T]); t1, b_t1 = sb(ph, "t1", [128, T])
                      w2f, b_w2f = sb(ph, "w2f", [128, 2, 128]); w2b, b_w2b = sb(ph, "w2b", [128, 2, 128], BF16)
                      cA, b_cA = sb(ph, "cA", [128, 4]); sums, b_sums = sb(ph, "sums", [128, 4, 2]); srt, b_srt = sb(ph, "srt", [128, 4])
                      h0s, b_h0s = sb(ph, "h0s", [128, 4, NS]); rall, b_rall = sb(ph, "rall", [128, NCORE, 8])
                      Hc, b_Hc = sb(ph, "Hc", [128, 4]); Ap, b_Ap = sb(ph, "Ap", [128, 4]); Hp, b_Hp = sb(ph, "Hp", [128, 4])
                      lo, b_lo = sb(ph, "lo", [128, 4]); los, b_los = sb(ph, "los", [128, 4, NS])
                      tl, b_tl = sb(ph, "tl", [128, 4, 3])
                      w, bw = load_w(wt, w_in[l, :, 0:512], KT, 512)
                      for j in range(4):
                          ps, bps = psd()
                          for k in range(KT):
                              mm(ps[:, 0:4], w[:, k, j * 128:(j + 1) * 128], hT[:, k, TP - 4:TP], k == 0, k == KT - 1, [bw, b_hT], [bps])
                          vcopy(tl[:, j, :], ps[:, 1:4], [bps], [b_tl])
                      dma(sendU.ap().rearrange("p (j e) -> p j e", e=3), tl[:], [b_tl], [b_sendU])
                      dma(o_conv_p[l].rearrange("(j p) e -> p j e", p=128), tl[:], [b_tl], [b_out])
                      allgather(sendU, recvU, b_sendU, b_recvU)
                      memset(utl[:], 0.0, [b_utl])
                      for c in range(NCORE):
                          dma(tmpU[:], recvU.ap()[c * 128:(c + 1) * 128, :], [b_recvU], [b_tmpU])
                          stt(utl[:], tmpU[:], onehot[:, c:c + 1], utl[:], ALU.mult, ALU.add, [b_tmpU, b_met, b_utl], [b_utl])
                      act(cA[:], V(V_LAM, 4), AF.Exp, [b_vec], [b_cA], scale=-1.0)
                      act(cA[:], cA[:], AF.Ln, [b_cA, b_epsc], [b_cA], bias=epsc[:, 1:2])
                      ts(cA[:], cA[:], -8.0, None, ALU.mult, ALU.bypass, [b_cA], [b_cA])
                      dma(h0s[:], st_lru[l].rearrange("(j p) s -> p j s", p=128), [], [b_h0s])
                      for j in range(4):
                          def ev_u(jj, c0, n, ps, bps):
                              if c0 < TP:
                                  evac_copy(uT[:, 3 + c0:3 + c0 + n], ps[:, 0:n], [bps], [b_uT])
                              else:
                                  evac_copy(uT[:, 3 + TP:UW].rearrange("p (s e) -> p s e", e=11)[:, :, 3:11],
                                            ps[:, 0:32].rearrange("p (s e) -> p s e", e=8), [bps], [b_uT])
                          proj_fm(w_in[l, :, j * 128:(j + 1) * 128], 128, hT, b_hT, KT, ev_u)
                          vcopy(uT[:, 0:3], utl[:, j * 3:j * 3 + 3], [b_utl], [b_uT])
                          for s in range(NS):
                              dma(uT[:, 3 + TP + 11 * s:3 + TP + 11 * s + 3], st_conv[l, s, j * 128:(j + 1) * 128, :], [], [b_uT])
                          for s in range(NS):
                              dma(o_conv_s[l, s, j * 128:(j + 1) * 128, :], uT[:, 3 + TP + 11 * s + 8:3 + TP + 11 * s + 11], [b_uT], [b_out])
                          uP = lambda tap: uT[:, tap:tap + TP]
                          uS = lambda tap: uT[:, 3 + TP:UW].rearrange("p (s e) -> p s e", e=11)[:, :, tap:tap + 8]
                          xcS = xc[:, TP:T].rearrange("p (s e) -> p s e", e=8)
                          act(xc[:, 0:TP], uP(0), AF.Identity, [b_uT, b_vec], [b_xc], bias=V(V_CB + j), scale=V(V_CW + j))
                          act(xcS, uS(0), AF.Identity, [b_uT, b_vec], [b_xc], bias=V(V_CB + j), scale=V(V_CW + j))
                          for tap in range(1, 4):
                              stt(xc[:, 0:TP], uP(tap), V(V_CW + tap * 4 + j), xc[:, 0:TP], ALU.mult, ALU.add, [b_uT, b_vec, b_xc], [b_xc])
                              stt(xcS, uS(tap), V(V_CW + tap * 4 + j), xcS, ALU.mult, ALU.add, [b_uT, b_vec, b_xc], [b_xc])
                          acopy(xcb[:], xc[:], [b_xc], [b_xcb])
                          for wi_, (wsrc, bcol) in enumerate(((lru_wr, V_BR), (lru_wi, V_BI))):
                              memset(w2f[:, wi_, :], 0.0, [b_w2f])
                              dma(w2f[0:64, wi_, 0:64], wsrc[l, 2 * j], [], [b_w2f])
                              dma(w2f[64:128, wi_, 64:128], wsrc[l, 2 * j + 1], [], [b_w2f])
                              vcopy(w2b[:, wi_, :], w2f[:, wi_, :], [b_w2f], [b_w2b])
                              for (c0, n) in BLOCKS:
                                  ps, bps = psd()
                                  mm(ps[:, 0:n], w2b[:, wi_, :], xcb[:, c0:c0 + n], True, True, [b_w2b, b_xcb], [bps])
                                  act(rg[:, c0:c0 + n], ps[:, 0:n], AF.Sigmoid, [bps, b_vec], [b_rg], bias=V(bcol + j))
                              if wi_ == 0:
                                  rsum(srt[:, j:j + 1], rg[:, 0:TP], [b_rg], [b_srt])
                                  act(aa[:, j, :], rg[:], AF.Exp, [b_rg, b_cA], [b_aa], scale=cA[:, j:j + 1])
                                  act(t1[:], aa[:, j, :], AF.Square, [b_aa], [b_t1])
                                  ts(t1[:], t1[:], -1.0, 1.0, ALU.mult, ALU.add, [b_t1], [b_t1])
                                  ts(t1[:], t1[:], 1e-30, None, ALU.max, ALU.bypass, [b_t1], [b_t1])
                                  tt(t1[:, 0:1], t1[:, 0:1], first_flag, ALU.max, [b_t1, b_met], [b_t1])
                                  act(t1[:], t1[:], AF.Sqrt, [b_t1], [b_t1])
                          tt(t1[:], t1[:], rg[:], ALU.mult, [b_t1, b_rg], [b_t1])
                          tt(bt[:, j, :], t1[:], xc[:], ALU.mult, [b_t1, b_xc], [b_bt])
                          scan(t1[:, 0:TP], aa[:, j, 0:TP], bt[:, j, 0:TP], 0.0, [b_aa, b_bt], [b_t1])
                          vcopy(sums[:, j, 1:2], t1[:, TP - 1:TP], [b_t1], [b_sums])
                          act(sums[:, j, 0:1], srt[:, j:j + 1], AF.Exp, [b_srt, b_cA], [b_sums], scale=cA[:, j:j + 1])
                      dma(sendS.ap().rearrange("p (j e) -> p j e", e=2), sums[:], [b_sums], [b_sendS])
                      allgather(sendS, recvS, b_sendS, b_recvS)
                      dma(rall[:], recvS.ap().rearrange("(c p) e -> p c e", p=128), [b_recvS], [b_rall])
                      memset(Hc[:], 0.0, [b_Hc])
                      rv4 = rall[:].rearrange("p c (j e) -> p c j e", e=2)
                      for c in range(NCORE):
                          ts(Ap[:], rv4[:, c, :, 0], pmask[:, c:c + 1], None, ALU.mult, ALU.bypass, [b_rall, b_met], [b_Ap])
                          ts(Ap[:], Ap[:], omp[:, c:c + 1], None, ALU.add, ALU.bypass, [b_Ap, b_omp], [b_Ap])
                          ts(Hp[:], rv4[:, c, :, 1], pmask[:, c:c + 1], None, ALU.mult, ALU.bypass, [b_rall, b_met], [b_Hp])
                          tt(Hc[:], Hc[:], Ap[:], ALU.mult, [b_Hc, b_Ap], [b_Hc])
                          tt(Hc[:], Hc[:], Hp[:], ALU.add, [b_Hc, b_Hp], [b_Hc])
                      for j in range(4):
                          scan(t1[:, 0:TP], aa[:, j, 0:TP], bt[:, j, 0:TP], Hc[:, j:j + 1], [b_aa, b_bt, b_Hc], [b_t1])
                          for s in range(NS):
                              cs = slice(TP + 8 * s, TP + 8 * s + 8)
                              scan(t1[:, cs], aa[:, j, cs], bt[:, j, cs], h0s[:, j, s:s + 1], [b_aa, b_bt, b_h0s], [b_t1])
                          acopy(yaT[:, j, :], t1[:], [b_t1], [b_ya])
                          vcopy(lo[:, j:j + 1], t1[:, TP - 1:TP], [b_t1], [b_lo])
                          vcopy(los[:, j, :], t1[:, TP + 7:T:8], [b_t1], [b_los])
                      dma(o_lru_p[l], lo[:], [b_lo], [b_out])
                      dma(o_lru_s[l], los[:], [b_los], [b_out])

                  stop(4)
                  ybT, b_yb = sb(lay, "ybT", [128, 4, T], BF16)
                  with ExitStack() as ph:
                      qt, b_qt = sb(ph, "qt", [128, 4, T], BF16); kt_, b_kt = sb(ph, "kt_", [128, 4, T], BF16)
                      Eh, b_Eh = sb(ph, "Eh", [128, 4, 36]); Dh, b_Dh = sb(ph, "Dh", [128, 4])
                      Sst, b_S = sb(ph, "Sst", [128, 4, 128]); Sbf, b_Sbf = sb(ph, "Sbf", [128, 4, 128], BF16)
                      with ExitStack() as pp:
                          qs_, b_qs = sb(pp, "qs_", [128, T]); sf, b_sf = sb(pp, "sf", [128, T]); lf, b_lf = sb(pp, "lf", [128, T])
                          bb, b_bb = sb(pp, "bb", [128, T]); eb, b_eb = sb(pp, "eb", [128, T])
                          sl, b_sl = sb(pp, "sl", [128, 1]); rmk, b_rmk = sb(pp, "rmk", [128, T])
                          dma(rmk[:], consts[:, C_RM:C_RM + T], [], [b_rmk])
                          for hd in range(4):
                              lb_ = lbv[:, l, 0, hd:hd + 1]; oml = lbv[:, l, 1, hd:hd + 1]; noml = lbv[:, l, 2, hd:hd + 1]

                              def ev_q(j, c0, n, ps, bps):
                                  act(qs_[:, c0:c0 + n], ps[:, 0:n], AF.Silu, [bps], [b_qs])

                              def ev_f(j, c0, n, ps, bps):
                                  act(sf[:, c0:c0 + n], ps[:, 0:n], AF.Sigmoid, [bps], [b_sf])
                              proj_fm(w_in[l, :, 512 + hd * 128:512 + (hd + 1) * 128], 128, hT, b_hT, KT, ev_q)
                              proj_fm(w_in[l, :, 1024 + hd * 128:1024 + (hd + 1) * 128], 128, hT, b_hT, KT, ev_f)
                              act(lf[:], sf[:], AF.Ln, [b_sf, b_lbv], [b_lf], bias=lb_, scale=oml)
                              act(sf[:], sf[:], AF.Identity, [b_sf, b_lbv], [b_sf], bias=oml, scale=noml)
                              scan(bb[:], rmk[:], lf[:], 0.0, [b_rmk, b_lf], [b_bb])
                              act(eb[:], bb[:], AF.Exp, [b_bb], [b_eb])
                              tt(qt[:, hd, :], qs_[:], eb[:], ALU.mult, [b_qs, b_eb], [b_qt])
                              vcopy(Eh[:, hd, 0:32], eb[:, 63:TP:64], [b_eb], [b_Eh])
                              vcopy(Eh[:, hd, 32:36], eb[:, TP + 7:T:8], [b_eb], [b_Eh])
                              act(eb[:], bb[:], AF.Exp, [b_bb], [b_eb], scale=-1.0)
                              tt(kt_[:, hd, :], sf[:], eb[:], ALU.mult, [b_sf, b_eb], [b_kt])
                              rsum(sl[:], lf[:, 0:TP], [b_lf], [b_sl])
                              act(Dh[:, hd:hd + 1], sl[:], AF.Exp, [b_sl], [b_Dh])
                              if DBGH and l == 0 and hd == 0:
                                  dma(o_dbg[:, 0, :], qs_[:], [b_qs], [b_out])
                                  dma(o_dbg[:, 1, :], sf[:], [b_sf], [b_out])
                                  dma(o_dbg[:, 2, :], lf[:], [b_lf], [b_out])
                                  dma(o_dbg[:, 3, :], bb[:], [b_bb], [b_out])
                                  dma(o_dbg[:, 4, :], eb[:], [b_eb], [b_out])
                                  pass
                      oT, b_oT = sb(ph, "oT", [128, 4, T])
                      wiv, b_wiv = sb(ph, "wiv", [128, KT, 512], BF16)
                      gdma(wiv[:], w_in[l, :, 1536:2048].rearrange("(k p) c -> p k c", p=128), [], [b_wiv])
                      vch, b_vch = sb(ph, "vch", [64, 512], BF16); ktok, b_ktok = sb(ph, "ktok", [64, 128], BF16)
                      att, b_att = sb(ph, "att", [64, 64], BF16); stmp, b_stmp = sb(ph, "stmp", [128, 128])
                      hall, b_hall = sb(ph, "hall", [128, 516]); Dp, b_Dp = sb(ph, "Dp", [128, 4]); Sp, b_Sp = sb(ph, "Sp", [128, 512])
                      cmb = cst[0:64, C_CM:C_CM + 64]

                      def chain(c0, L, ci, full):
                          ps, bps = psd()
                          for k in range(KT):
                              mm(ps[0:L, :], hT[:, k, c0:c0 + L], wiv[:, k, :], k == 0, k == KT - 1, [b_hT, b_wiv], [bps])
                          acopy(vch[0:L, :], ps[0:L, :], [bps], [b_vch])
                          for hd in range(4):
                              tr(psT[0:L, 0:128], kt_[:, hd, c0:c0 + L], identb[:], [b_kt, b_identb], [b_psT])
                              vcopy(ktok[0:L, :], psT[0:L, 0:128], [b_psT], [b_ktok])
                              pg, bpg = psd()
                              mm(pg[:, 0:128], ktok[0:L, :], vch[0:L, hd * 128:(hd + 1) * 128], True, True, [b_ktok, b_vch], [bpg])
                              if full:
                                  pa, bpa = psd()
                                  mm(pa[0:L, 0:L], kt_[:, hd, c0:c0 + L], qt[:, hd, c0:c0 + L], True, True, [b_kt, b_qt], [bpa])
                                  tt(att[0:L, 0:L], pa[0:L, 0:L], cmb[0:L, 0:L], ALU.mult, [bpa, b_cst], [b_att])
                                  po_, bpo = psd()
                                  mm(po_[:, 0:L], vch[0:L, hd * 128:(hd + 1) * 128], att[0:L, 0:L], True, False, [b_vch, b_att], [bpo])
                                  mm(po_[:, 0:L], Sbf[:, hd, :], qt[:, hd, c0:c0 + L], False, True, [b_Sbf, b_qt], [bpo])
                                  acopy(oT[:, hd, c0:c0 + L], po_[:, 0:L], [bpo], [b_oT])
                              tt(stmp[:], pg[:, 0:128], Sst[:, hd, :], ALU.add, [bpg, b_S], [b_stmp])
                              ts(Sst[:, hd, :], stmp[:], Eh[:, hd, ci:ci + 1], None, ALU.mult, ALU.bypass, [b_stmp, b_Eh], [b_S])
                              if full:
                                  acopy(Sbf[:, hd, :], Sst[:, hd, :], [b_S], [b_Sbf])

                      memset(Sst[:], 0.0, [b_S])
                      for ci in range(32):
                          chain(ci * 64, 64, ci, False)
                          if DBGH and l == 0 and ci in (0, 1, 4):
                              dma(o_dbg[:, 6, {0: 0, 1: 512, 4: 1024}[ci]:{0: 0, 1: 512, 4: 1024}[ci] + 512], Sst[:].rearrange("p h v -> p (h v)"), [b_S], [b_out])
                      if DBGH and l == 0:
                          dma(o_dbg[:, 5, 0:512], Sst[:].rearrange("p h v -> p (h v)"), [b_S], [b_out])
                          dma(o_dbg[:, 5, 512:516], Dh[:], [b_Dh], [b_out])
                          dma(o_dbg[:, 5, 600:636], Eh[:, 0, :], [b_Eh], [b_out])
                      memset(Sp[:, 0:128], 0.0, [b_Sp])
                      vcopy(Sp[:, 0:4], Dh[:], [b_Dh], [b_Sp])
                      for hd in range(4):
                          dma(sendH[hd].ap(), Sst[:, hd, :], [b_S], [b_sendH])
                      dma(sendH[4].ap(), Sp[:, 0:128], [b_Sp], [b_sendH])
                      for i_ in range(5):
                          allgather(sendH[i_], recvH[i_], b_sendH, b_recvH)
                      memset(Sst[:], 0.0, [b_S])
                      for c in range(NCORE):
                          for i_ in range(5):
                              dma(hall[:, i_ * 128:i_ * 128 + (128 if i_ < 4 else 4)], recvH[i_].ap()[c * 128:(c + 1) * 128, 0:(128 if i_ < 4 else 4)], [b_recvH], [b_hall])
                          ts(Dp[:], hall[:, 512:516], pmask[:, c:c + 1], None, ALU.mult, ALU.bypass, [b_hall, b_met], [b_Dp])
                          ts(Dp[:], Dp[:], omp[:, c:c + 1], None, ALU.add, ALU.bypass, [b_Dp, b_omp], [b_Dp])
                          ts(Sp[:], hall[:, 0:512], pmask[:, c:c + 1], None, ALU.mult, ALU.bypass, [b_hall, b_met], [b_Sp])
                          for hd in range(4):
                              stt(Sst[:, hd, :], Sst[:, hd, :], Dp[:, hd:hd + 1], Sp[:, hd * 128:(hd + 1) * 128], ALU.mult, ALU.add, [b_S, b_Dp, b_Sp], [b_S])
                      acopy(Sbf[:], Sst[:], [b_S], [b_Sbf])
                      for ci in range(32):
                          chain(ci * 64, 64, ci, True)
                      dma(o_hg_p[l].rearrange("h d v -> d h v"), Sst[:], [b_S], [b_out])
                      if DBGH and l == 0:
                          dma(o_dbg[:, 7, :], oT[:, 0, :], [b_oT], [b_out])
                      for s in range(NS):
                          dma(Sst[:], st_hg[l, s].rearrange("h d v -> d h v"), [], [b_S])
                          acopy(Sbf[:], Sst[:], [b_S], [b_Sbf])
                          chain(TP + 8 * s, 8, 32 + s, True)
                          dma(o_hg_s[l, s].rearrange("h d v -> d h v"), Sst[:], [b_S], [b_out])
                      sq = sb(ph, "sqh", [128, 512]); rv = sb(ph, "rvh", [128, 512]); sgt, b_sgt = sb(ph, "sgt", [128, 512])
                      for hd in range(4):
                          for (c0, n) in BLOCKS:
                              ps, bps = psd()
                              act(sq[0][:, 0:n], oT[:, hd, c0:c0 + n], AF.Square, [b_oT], [sq[1]])
                              mm(ps[:, 0:n], onesf[:], sq[0][:, 0:n], True, True, [b_onesf, sq[1]], [bps])
                              act(rv[0][:, 0:n], ps[:, 0:n], AF.Sqrt, [bps, b_epsc], [rv[1]], bias=epsc[:, 0:1], scale=1.0 / 128)
                              recip(rv[0][:, 0:n], rv[0][:, 0:n], [rv[1]], [rv[1]])
                              stt(oT[:, hd, c0:c0 + n], oT[:, hd, c0:c0 + n], V(V_HN), rv[0][:, 0:n], ALU.mult, ALU.mult, [b_oT, b_vec, rv[1]], [b_oT])

                          def ev_g(j, c0, n, ps, bps, hd=hd):
                              act(sgt[:, 0:n], ps[:, 0:n], AF.Silu, [bps], [b_sgt])
                              tt(ybT[:, hd, c0:c0 + n], oT[:, hd, c0:c0 + n], sgt[:, 0:n], ALU.mult, [b_oT, b_sgt], [b_yb])
                          proj_fm(w_in[l, :, 2048 + hd * 128:2048 + (hd + 1) * 128], 128, hT, b_hT, KT, ev_g)

                  stop(5)
                  with ExitStack() as ph:
                      mixT, b_mix = sb(ph, "mixT", [128, KT, T], BF16)
                      with ExitStack() as pp:
                          acc, b_acc = sb(pp, "acc", [128, T]); gsb, b_gsb = sb(pp, "gsb", [128, 512]); tm, b_tm = sb(pp, "tm", [128, 512])
                          wg = [sb(pp, "wg%d" % i, [128, KT, 128], BF16) for i in range(2)]
                          wb = [sb(pp, "wb%d" % i, [128, 4, 128], BF16) for i in range(2)]
                          brs = ((w_bra, yaT, b_ya, 4), (w_brb, ybT, b_yb, 4), (w_brc, ycT, b_yc, 2))
                          for m in range(8):
                              for bi, (wsrc, yT, byT, kt2) in enumerate(brs):
                                  wgt, bwg = load_w(wg, w_in[l, :, 4864 + bi * 1024 + m * 128:4864 + bi * 1024 + (m + 1) * 128], KT, 128)
                                  wbt, bwb = load_w(wb, wsrc[l, :, m * 128:(m + 1) * 128], kt2, 128)
                                  for (c0, n) in BLOCKS:
                                      ps, bps = psd()
                                      for k in range(KT):
                                          mm(ps[:, 0:n], wgt[:, k, :], hT[:, k, c0:c0 + n], k == 0, k == KT - 1, [bwg, b_hT], [bps])
                                      act(gsb[:, 0:n], ps[:, 0:n], AF.Sigmoid, [bps, b_vec], [b_gsb], bias=V(V_BG + bi * 8 + m))
                                      ps2, bps2 = psd()
                                      for k in range(kt2):
                                          mm(ps2[:, 0:n], wbt[:, k, :], yT[:, k, c0:c0 + n], k == 0, k == kt2 - 1, [bwb, byT], [bps2])
                                      if bi == 0:
                                          tt(acc[:, c0:c0 + n], ps2[:, 0:n], gsb[:, 0:n], ALU.mult, [bps2, b_gsb], [b_acc])
                                      else:
                                          tt(tm[:, 0:n], ps2[:, 0:n], gsb[:, 0:n], ALU.mult, [bps2, b_gsb], [b_tm])
                                          tt(acc[:, c0:c0 + n], acc[:, c0:c0 + n], tm[:, 0:n], ALU.add, [b_acc, b_tm], [b_acc])
                              acopy(mixT[:, m, :], acc[:], [b_acc], [b_mix])
                      wo, b_wo = sb(ph, "wo", [128, KT, D], BF16)
                      gdma(wo[:, :, 0:512], w_out[l, :, 0:512].rearrange("(k p) c -> p k c", p=128), [], [b_wo])
                      gdma(wo[:, :, 512:D], w_out[l, :, 512:D].rearrange("(k p) c -> p k c", p=128), [], [b_wo])
                      outp, b_outp = sb(ph, "outp", [128, KT, 512])
                      xb, b_xb = sb(ph, "xb2", [128, KT, 512]); sq = sb(ph, "sq2", [128, 512]); rv = sb(ph, "rv2", [128, 512])
                      for (c0, n) in BLOCKS:
                          for m in range(8):
                              ps, bps = psd()
                              for k in range(KT):
                                  mm(ps[:, 0:n], wo[:, k, m * 128:(m + 1) * 128], mixT[:, k, c0:c0 + n], k == 0, k == KT - 1, [b_wo, b_mix], [bps])
                              evac_copy(outp[:, m, 0:n], ps[:, 0:n], [bps], [b_outp])
                          r, br_ = rms_rinv(ph, outp, b_outp, KT, 0, n, {"sq": sq, "rv": rv}, 1.0 / D)
                          dma(xb[:, :, 0:n], xTv[:, :, c0:c0 + n], [b_xT], [b_xb])
                          for k in range(KT):
                              stt(outp[:, k, 0:n], outp[:, k, 0:n], V(V_NPOST + k), r[:, 0:n], ALU.mult, ALU.mult, [b_outp, b_vec, br_], [b_outp])
                              tt(xb[:, k, 0:n], xb[:, k, 0:n], outp[:, k, 0:n], ALU.add, [b_xb, b_outp], [b_xb])
                          dma(xTv[:, :, c0:c0 + n], xb[:, :, 0:n], [b_xb], [b_xT])
                          r, br_ = rms_rinv(ph, xb, b_xb, KT, 0, n, {"sq": sq, "rv": rv}, 1.0 / D)
                          for k in range(KT):
                              stt(hT[:, k, c0:c0 + n], xb[:, k, 0:n], V(V_MPRE + k), r[:, 0:n], ALU.mult, ALU.mult, [b_xb, b_vec, br_], [b_hT])

              stop(6)
              with ExitStack() as ph:
                  outp, b_outp = sb(ph, "outm", [128, KT, T])
                  ac, b_ac = sb(ph, "ac", [128, 4, T], BF16)
                  wd = [sb(ph, "wd%d" % i, [128, 4, D], BF16) for i in range(2)]
                  t1, b_t1 = sb(ph, "t1m", [128, 512])
                  for c in range(8):
                      def ev_a(j, c0, n, ps, bps):
                          act(t1[:, 0:n], ps[:, 0:n], AF.Relu, [bps], [b_t1])
                          tt(ac[:, j, c0:c0 + n], t1[:, 0:n], t1[:, 0:n], ALU.mult, [b_t1], [b_ac])
                      proj_fm(w_up[l, :, c * 512:(c + 1) * 512], 512, hT, b_hT, KT, ev_a)
                      wdt, bwd = wd[c % 2]
                      gdma(wdt[:, :, 0:512], w_down[l, c * 512:(c + 1) * 512, 0:512].rearrange("(k p) c -> p k c", p=128), [], [bwd])
                      gdma(wdt[:, :, 512:D], w_down[l, c * 512:(c + 1) * 512, 512:D].rearrange("(k p) c -> p k c", p=128), [], [bwd])
                      for m in range(8):
                          for (c0, n) in BLOCKS:
                              ps, bps = psd()
                              for k in range(4):
                                  mm(ps[:, 0:n], wdt[:, k, m * 128:(m + 1) * 128], ac[:, k, c0:c0 + n], k == 0, k == 3, [bwd, b_ac], [bps])
                              if c == 0:
                                  evac_copy(outp[:, m, c0:c0 + n], ps[:, 0:n], [bps], [b_outp])
                              else:
                                  tt(outp[:, m, c0:c0 + n], ps[:, 0:n], outp[:, m, c0:c0 + n], ALU.add, [bps, b_outp], [b_outp])
                  xb, b_xb = sb(ph, "xb3", [128, KT, 512]); sq = sb(ph, "sq3", [128, 512]); rv = sb(ph, "rv3", [128, 512])
                  yt, b_yt = sb(ph, "yt", [128, D])
                  for (c0, n) in BLOCKS:
                      r, br_ = rms_rinv(ph, outp, b_outp, KT, c0, n, {"sq": sq, "rv": rv}, 1.0 / D)
                      dma(xb[:, :, 0:n], xTv[:, :, c0:c0 + n], [b_xT], [b_xb])
                      for k in range(KT):
                          stt(outp[:, k, c0:c0 + n], outp[:, k, c0:c0 + n], V(V_MPOST + k), r[:, 0:n], ALU.mult, ALU.mult, [b_outp, b_vec, br_], [b_outp])
                          tt(xb[:, k, 0:n], xb[:, k, 0:n], outp[:, k, c0:c0 + n], ALU.add, [b_xb, b_outp], [b_xb])
                      if l == 0:
                          dma(xTv[:, :, c0:c0 + n], xb[:, :, 0:n], [b_xb], [b_xT])
                      else:
                          for t0 in range(0, n, 128):
                              nt = min(128, n - t0)
                              for k in range(KT):
                                  tr(psA[0:nt, k * 128:(k + 1) * 128], xb[:, k, t0:t0 + nt], identf, [b_xb, b_cst], [b_psA])
                              evac_copy(yt[0:nt, :], psA[0:nt, :], [b_psA], [b_yt])
                              dma(o_y[c0 + t0:c0 + t0 + nt, :], yt[0:nt, :], [b_yt], [b_out])

        except StopBuild:
            pass

        finalize(nc, H)
        print("instr counts:", {k: len(v.ops) for k, v in S.items()})
    return nc


def make_consts():
    c = np.zeros((128, NCONST), np.float32)
    j = np.arange(128)[:, None]; q = np.arange(128)[None, :]
    c[:, C_MASK:C_MASK + 128] = (j >= q)
    c[:, C_MASK + 128:C_MASK + 256] = (j <= q)
    s = np.arange(64)[:, None]; t = np.arange(64)[None, :]
    c[0:64, C_CM:C_CM + 64] = (t >= s)
    rm = np.ones(T, np.float32); rm[0:TP:64] = 0; rm[TP:T:8] = 0
    c[:, C_RM:C_RM + T] = rm[None, :]
    p = np.arange(128)[:, None]; tq = np.arange(8)[None, :]
    for g in range(3):
        d = DIL[g]
        same = (p % d) == (tq % d)
        c[:, C_SM + (g * 3 + 0) * 8:C_SM + (g * 3 + 0) * 8 + 8] = same & (p >= tq)
        c[:, C_SM + (g * 3 + 1) * 8:C_SM + (g * 3 + 1) * 8 + 8] = same
        c[:, C_SM + (g * 3 + 2) * 8:C_SM + (g * 3 + 2) * 8 + 8] = same & (p <= tq) & (p < 8)
    c[:, C_ID:C_ID + 128] = np.eye(128, dtype=np.float32)
    return c


def fm(v, kt):
    return np.ascontiguousarray(np.asarray(v, np.float32).reshape(kt, 128).T)


_NC = None


def kernel(x_prompt, x_sample, state_conv, state_lru, state_hgrn, cache_win128, cache_win512, cache_win2048,
           norm_mix_pre, norm_mix_post, norm_mlp_pre, norm_mlp_post, w_in, b_gate, conv_w, conv_b,
           lru_wr, lru_br, lru_wi, lru_bi, lru_lambda, hgrn_lb_raw, hgrn_norm, w_br_a, w_br_b, w_br_c,
           w_out, w_mlp_up, w_mlp_down):
    global _NC
    f = lambda a: np.ascontiguousarray(np.asarray(a, np.float32))
    vecs = np.zeros((2, 128, NVEC), np.float32)
    for l in range(2):
        vecs[l, :, V_NPRE:V_NPRE + 8] = fm(norm_mix_pre[l], 8)
        vecs[l, :, V_NPOST:V_NPOST + 8] = fm(norm_mix_post[l], 8)
        vecs[l, :, V_MPRE:V_MPRE + 8] = fm(norm_mlp_pre[l], 8)
        vecs[l, :, V_MPOST:V_MPOST + 8] = fm(norm_mlp_post[l], 8)
        vecs[l, :, V_BG:V_BG + 24] = fm(b_gate[l], 24)
        for tap in range(4):
            vecs[l, :, V_CW + tap * 4:V_CW + tap * 4 + 4] = fm(np.asarray(conv_w)[l, tap], 4)
        vecs[l, :, V_CB:V_CB + 4] = fm(conv_b[l], 4)
        vecs[l, :, V_BR:V_BR + 4] = fm(lru_br[l], 4)
        vecs[l, :, V_BI:V_BI + 4] = fm(lru_bi[l], 4)
        vecs[l, :, V_LAM:V_LAM + 4] = fm(lru_lambda[l], 4)
        vecs[l, :, V_LB:V_LB + 4] = fm(np.asarray(hgrn_lb_raw)[0], 4)
        vecs[l, :, V_LB + 4:V_LB + 8] = fm(np.asarray(hgrn_lb_raw)[1], 4)
        vecs[l, :, V_HN] = np.asarray(hgrn_norm, np.float32)[l]
    consts = make_consts()
    xp = f(x_prompt)[0]; xsm = f(x_sample)
    caches = [f(c).reshape(2, 32, W, 512) for c, W in zip((cache_win128, cache_win512, cache_win2048), WIN)]
    shared = {"w_in": f(w_in), "w_bra": f(w_br_a), "w_brb": f(w_br_b), "w_brc": f(w_br_c), "w_out": f(w_out),
              "w_up": f(w_mlp_up), "w_down": f(w_mlp_down), "lru_wr": f(lru_wr), "lru_wi": f(lru_wi),
              "vecs": vecs, "consts": consts}
    in_maps = []
    for c in range(NCORE):
        sq = slice(NS * c, NS * (c + 1))
        meta = np.zeros((128, 24), np.float32)
        if c > 0:
            meta[:, c - 1] = 1.0
            meta[:, 16] = 1.0
        meta[:, 8:8 + c] = 1.0
        if c == 0:
            meta[:, 17] = 1.0
        m = dict(shared)
        m.update({"xp": np.ascontiguousarray(xp[c * TP:(c + 1) * TP]), "xs": np.ascontiguousarray(xsm[sq].reshape(NS * TS, D)),
                  "st_conv": np.ascontiguousarray(f(state_conv)[:, sq].transpose(0, 1, 3, 2)),
                  "st_lru": np.ascontiguousarray(f(state_lru)[:, sq].transpose(0, 2, 1)),
                  "st_hg": np.ascontiguousarray(f(state_hgrn)[:, sq]), "meta": meta})
        for g in range(3):
            m["cache%d" % g] = np.ascontiguousarray(caches[g][:, sq])
        in_maps.append(m)
    if _NC is None:
        _NC = build_nc()
    res = run_bass_kernel_spmd(_NC, in_maps, core_ids=list(range(NCORE)))
    R = res.results
    _LAST[0] = R
    yp = np.concatenate([R[c]["o_y"][0:TP] for c in range(NCORE)], 0)[None]
    ys = np.concatenate([R[c]["o_y"][TP:T].reshape(NS, TS, D) for c in range(NCORE)], 0)
    L = NCORE - 1
    conv_p = R[L]["o_conv_p"].transpose(0, 2, 1)[:, None]
    conv_s = np.concatenate([R[c]["o_conv_s"].transpose(0, 1, 3, 2) for c in range(NCORE)], 1)
    lru_p = R[L]["o_lru_p"].transpose(0, 2, 1).reshape(2, 512)[:, None]
    lru_s = np.concatenate([R[c]["o_lru_s"].transpose(0, 3, 2, 1).reshape(2, NS, 512) for c in range(NCORE)], 1)
    hg_p = R[L]["o_hg_p"][:, None]
    hg_s = np.concatenate([R[c]["o_hg_s"] for c in range(NCORE)], 1)
    outs = [yp, ys, conv_p, conv_s, lru_p, lru_s, hg_p, hg_s]
    for g in range(3):
        outs.append(R[L]["o_win_p%d" % g].reshape(2, 1, WIN[g], 2, 4, 64))
        outs.append(np.concatenate([R[c]["o_win_s%d" % g] for c in range(NCORE)], 1).reshape(2, 32, WIN[g], 2, 4, 64))
    return tuple(np.ascontiguousarray(o.astype(np.float32)) for o in outs)
```
